# Optimizing a Trainium2 kernel written in Bass

```python
import jax, jax.numpy as jnp
from jax import lax
import numpy as np


D_MODEL = 2048
BATCH = 4
SEQ = 2048
DEPTH = 2

CHUNK = 64
N_META = 16
N_A_LAYERS = max(1, DEPTH // 2)
N_B_LAYERS = DEPTH - N_A_LAYERS
RW_HEAD = 64
RW_HEADS = D_MODEL // RW_HEAD
DECAY_LORA = 96
AAA_LORA = 96
GATE_LORA = 256
RW_GN_EPS = RW_HEAD * 1e-5
FX_HEAD = 128
FX_HEADS = D_MODEL // FX_HEAD
Q_BLOCK = 128
D_FF = 5632
LN_EPS = 1e-5
ALPHA = (2 * DEPTH) ** 0.25
BETA = (8 * DEPTH) ** -0.25
NEG_INF = -1e30

kernel_name = 'hybrid_rwkv7_fox_yoco_encoder'


def layer_norm(x, g, b):
    xf = x.astype(jnp.float32)
    mu = jnp.mean(xf, axis=-1, keepdims=True)
    var = jnp.mean(jnp.square(xf - mu), axis=-1, keepdims=True)
    y = (xf - mu) * lax.rsqrt(var + LN_EPS)
    return (y * g + b).astype(x.dtype)


def swiglu(x, w1, w3, w2):
    return (jax.nn.silu(x @ w1) * (x @ w3)) @ w2


def wkv7_scan(r, decay, k, v, kk, a):
    B, L, H, N = r.shape
    xs = tuple(jnp.moveaxis(t, 1, 0) for t in (r, decay, k, v, kk, a))

    def step(S, inp):
        r_t, w_t, k_t, v_t, kk_t, a_t = inp
        sa = jnp.einsum('bhvk,bhk->bhv', S, -kk_t)
        S = (S * w_t[:, :, None, :]
             + sa[..., None] * (kk_t * a_t)[:, :, None, :]
             + v_t[..., None] * k_t[:, :, None, :])
        y = jnp.einsum('bhvk,bhk->bhv', S, r_t)
        return S, y

    S0 = jnp.zeros((B, H, N, N), jnp.float32)
    _, ys = lax.scan(step, S0, xs)
    return jnp.moveaxis(ys, 0, 1)


def rwkv7_mix(x, mu, w_rkv, w_o, w0, w1, w2, a0, a1, a2, g1, g2, k_k, k_a, r_k, lnx_g, lnx_b):
    B, L, D = x.shape
    xx = jnp.pad(x, ((0, 0), (1, 0), (0, 0)))[:, :-1] - x
    xr, xw, xk, xv, xa, xg = [x + xx * mu[i] for i in range(6)]
    rkv = jnp.einsum('nbld,nde->nble', jnp.stack([xr, xk, xv]), w_rkv)
    r, k, v = rkv[0], rkv[1], rkv[2]
    w = -jax.nn.softplus(-(w0 + jnp.tanh(xw @ w1) @ w2)) - 0.5
    a = jax.nn.sigmoid(a0 + (xa @ a1) @ a2)
    g = jax.nn.sigmoid(xg @ g1) @ g2
    hs = (B, L, RW_HEADS, RW_HEAD)
    f32 = jnp.float32
    kk = (k * k_k).astype(f32).reshape(hs)
    kk = kk / jnp.maximum(jnp.sqrt(jnp.sum(kk * kk, axis=-1, keepdims=True)), 1e-12)
    k = (k * (1.0 + (a - 1.0) * k_a)).astype(f32).reshape(hs)
    r = r.astype(f32).reshape(hs)
    v = v.astype(f32).reshape(hs)
    a = a.astype(f32).reshape(hs)
    decay = jnp.exp(-jnp.exp(w.astype(f32))).reshape(hs)
    y = wkv7_scan(r, decay, k, v, kk, a)
    mu_y = jnp.mean(y, axis=-1, keepdims=True)
    var_y = jnp.mean(jnp.square(y - mu_y), axis=-1, keepdims=True)
    yn = ((y - mu_y) * lax.rsqrt(var_y + RW_GN_EPS)).reshape(B, L, D) * lnx_g + lnx_b
    bonus = (jnp.sum(r * k * r_k, axis=-1, keepdims=True) * v).reshape(B, L, D)
    out = ((yn + bonus) * g).astype(x.dtype)
    return out @ w_o


def shared_kv(h, w_kvf, b_f):
    B, L, D = h.shape
    z = h @ w_kvf
    k = z[..., :D].reshape(B, L, FX_HEADS, FX_HEAD)
    v = z[..., D:2 * D].reshape(B, L, FX_HEADS, FX_HEAD)
    log_f = jax.nn.log_sigmoid(z[..., 2 * D:].astype(jnp.float32) + b_f)
    c = jnp.transpose(jnp.cumsum(log_f, axis=1), (0, 2, 1))
    return k, v, c


def fox_attention(x, w_q, w_o, k, v, c):
    B, L, D = x.shape
    q = (x @ w_q).reshape(B, L, FX_HEADS, FX_HEAD)
    scale = FX_HEAD ** -0.5
    outs = []
    for qs in range(0, L, Q_BLOCK):
        qe = min(qs + Q_BLOCK, L)
        s = jnp.einsum('bthd,bshd->bhts', q[:, qs:qe], k[:, :qe]).astype(jnp.float32) * scale
        s = s + c[:, :, qs:qe, None] - c[:, :, None, :qe]
        causal = jnp.arange(qe)[None, :] <= jnp.arange(qs, qe)[:, None]
        s = jnp.where(causal[None, None], s, NEG_INF)
        p = jax.nn.softmax(s, axis=-1).astype(v.dtype)
        outs.append(jnp.einsum('bhts,bshd->bthd', p, v[:, :qe]))
    o = jnp.concatenate(outs, axis=1).reshape(B, L, D)
    return o @ w_o


def setup_inputs(seed: int = 0) -> dict:
    key = jax.random.key(seed)
    ks = jax.random.split(key, 28)
    D, F, NA, NB = D_MODEL, D_FF, N_A_LAYERS, N_B_LAYERS

    def nrm(k, shape, scale):
        return jax.random.normal(k, shape, jnp.float32) * scale

    return {
        'x': nrm(ks[0], (BATCH, SEQ, D), 1.0),
        'meta_tokens': nrm(ks[1], (N_META, D), 1.0),
        'ln_g': 1.0 + nrm(ks[2], (DEPTH, 3, D), 0.02),
        'ln_b': nrm(ks[3], (DEPTH, 3, D), 0.02),
        'ffn_w1': nrm(ks[4], (DEPTH, 2, D, F), D ** -0.5),
        'ffn_w3': nrm(ks[5], (DEPTH, 2, D, F), D ** -0.5),
        'ffn_w2': nrm(ks[6], (DEPTH, 2, F, D), BETA * F ** -0.5),
        'rw_mu': jax.random.uniform(ks[7], (NA, 6, D), jnp.float32),
        'rw_w_rkv': nrm(ks[8], (NA, 3, D, D), D ** -0.5),
        'rw_w_o': nrm(ks[9], (NA, D, D), BETA * D ** -0.5),
        'rw_w0': jax.random.uniform(ks[10], (NA, D), jnp.float32, minval=-6.5, maxval=-1.5),
        'rw_w1': nrm(ks[11], (NA, D, DECAY_LORA), D ** -0.5),
        'rw_w2': nrm(ks[12], (NA, DECAY_LORA, D), 0.1 * DECAY_LORA ** -0.5),
        'rw_a0': nrm(ks[13], (NA, D), 0.1),
        'rw_a1': nrm(ks[14], (NA, D, AAA_LORA), D ** -0.5),
        'rw_a2': nrm(ks[15], (NA, AAA_LORA, D), AAA_LORA ** -0.5),
        'rw_g1': nrm(ks[16], (NA, D, GATE_LORA), D ** -0.5),
        'rw_g2': nrm(ks[17], (NA, GATE_LORA, D), GATE_LORA ** -0.5),
        'rw_k_k': 0.85 + nrm(ks[18], (NA, D), 0.02),
        'rw_k_a': 1.0 + nrm(ks[19], (NA, D), 0.02),
        'rw_r_k': nrm(ks[20], (NA, RW_HEADS, RW_HEAD), 0.1),
        'rw_lnx_g': 1.0 + nrm(ks[21], (NA, D), 0.02),
        'rw_lnx_b': nrm(ks[22], (NA, D), 0.02),
        'fx_w_q': nrm(ks[23], (NB, D, D), D ** -0.5),
        'fx_w_o': nrm(ks[24], (NB, D, D), BETA * D ** -0.5),
        'fx_w_kvf': nrm(ks[25], (D, 2 * D + FX_HEADS), D ** -0.5),
        'fx_b_f': jax.random.uniform(ks[26], (FX_HEADS,), jnp.float32, minval=1.0, maxval=4.0),
    }


def reference(x, meta_tokens, ln_g, ln_b, ffn_w1, ffn_w3, ffn_w2,
              rw_mu, rw_w_rkv, rw_w_o, rw_w0, rw_w1, rw_w2, rw_a0, rw_a1, rw_a2,
              rw_g1, rw_g2, rw_k_k, rw_k_a, rw_r_k, rw_lnx_g, rw_lnx_b,
              fx_w_q, fx_w_o, fx_w_kvf, fx_b_f):
    B = x.shape[0]
    meta = jnp.broadcast_to(meta_tokens.astype(x.dtype)[None], (B, N_META, x.shape[-1]))
    h = jnp.concatenate([meta, x], axis=1)
    k_s = v_s = c_s = None
    for l in range(DEPTH):
        h = layer_norm(ALPHA * h + 0.5 * swiglu(h, ffn_w1[l, 0], ffn_w3[l, 0], ffn_w2[l, 0]),
                       ln_g[l, 0], ln_b[l, 0])
        if l < N_A_LAYERS:
            mix = rwkv7_mix(h, rw_mu[l], rw_w_rkv[l], rw_w_o[l], rw_w0[l], rw_w1[l], rw_w2[l],
                            rw_a0[l], rw_a1[l], rw_a2[l], rw_g1[l], rw_g2[l],
                            rw_k_k[l], rw_k_a[l], rw_r_k[l], rw_lnx_g[l], rw_lnx_b[l])
        else:
            j = l - N_A_LAYERS
            mix = fox_attention(h, fx_w_q[j], fx_w_o[j], k_s, v_s, c_s)
        h = layer_norm(ALPHA * h + mix, ln_g[l, 1], ln_b[l, 1])
        h = layer_norm(ALPHA * h + 0.5 * swiglu(h, ffn_w1[l, 1], ffn_w3[l, 1], ffn_w2[l, 1]),
                       ln_g[l, 2], ln_b[l, 2])
        if l == N_A_LAYERS - 1:
            k_s, v_s, c_s = shared_kv(h, fx_w_kvf, fx_b_f)
    return h[:, N_META:]
```

```python
from contextlib import ExitStack
import numpy as np
import ml_dtypes
import concourse.bass as bass
import concourse.mybir as mybir
from concourse.bass_utils import run_bass_kernel_spmd

F32 = mybir.dt.float32
BF16 = mybir.dt.bfloat16
AF = mybir.ActivationFunctionType
ALU = mybir.AluOpType
NPBF = ml_dtypes.bfloat16

D = 2048
DT = 16
FF = 5632
FT = 44
B = 4
SEQ = 2048
NMETA = 16
L = SEQ + NMETA
T = L // 2
NC = 8
DEPTH = 2
ALPHA = (2 * DEPTH) ** 0.25
LN_EPS = 1e-5
RW_H = 32
RW_N = 64
GN_EPS = 64 * 1e-5
FX_H = 16
FX_N = 128
DEC = 0.6065306597126334


class Op:
    __slots__ = ("eng", "fn", "deps", "marked", "count", "sem", "is_dma")

    def __init__(self, eng, fn, is_dma=False):
        self.eng = eng
        self.fn = fn
        self.deps = []
        self.marked = False
        self.count = None
        self.sem = None
        self.is_dma = is_dma


class Prog:
    ENGS = ("pe", "act", "dve", "pool", "sp")

    def __init__(self, nc):
        self.nc = nc
        self.q = {e: [] for e in self.ENGS}
        self.last_w = {}
        self.readers = {}
        self.dma_slots = {}
        self.stack = ExitStack()
        self.all_ops = []
        self.cnt = {}

    def sbuf(self, name, shape, dt):
        return self.stack.enter_context(self.nc.sbuf_tensor(name, list(shape), dt))

    def psum(self, name, shape, dt=F32):
        return self.stack.enter_context(self.nc.psum_tensor(name, list(shape), dt))

    def rot(self, name, n=2):
        c = self.cnt.get(name, 0)
        self.cnt[name] = c + 1
        return c % n

    def _track(self, op, reads, writes):
        deps = op.deps
        for k in reads:
            w = self.last_w.get(k)
            if w is not None:
                deps.append((w, "raw"))
            self.readers.setdefault(k, []).append(op)
        for k in writes:
            w = self.last_w.get(k)
            if w is not None:
                deps.append((w, "waw"))
            for r in self.readers.get(k, ()):
                if r is not op:
                    deps.append((r, "war"))
            self.readers[k] = []
            self.last_w[k] = op

    def op(self, eng, fn, reads=(), writes=()):
        o = Op(eng, fn)
        self._track(o, reads, writes)
        self.q[eng].append(o)
        self.all_ops.append(o)
        return o

    def dma(self, eng, fn, slot, reads=(), writes=()):
        o = Op(eng, fn, is_dma=True)
        self._track(o, reads, writes)
        st = self.dma_slots.setdefault(slot, [None, 0])
        if st[0] is not None:
            o.deps.append((st[0], "slot"))
        st[0] = o
        st[1] += 1
        o.count = 16 * st[1]
        o.sem = slot
        o.marked = True
        self.q[eng].append(o)
        self.all_ops.append(o)
        return o

    def barrier(self):
        lasts = []
        for e in self.ENGS:
            for o in reversed(self.q[e]):
                if not o.is_dma:
                    lasts.append(o)
                    break
        seen = {}
        for o in self.all_ops:
            if o.is_dma:
                seen[o.sem] = o
        lasts += list(seen.values())
        for e in self.ENGS:
            o = Op(e, lambda eng: eng.nop())
            o.deps = [(d, "bar") for d in lasts if not (d.eng == e and not d.is_dma)]
            self.q[e].append(o)
            self.all_ops.append(o)

    def finish(self, keys):
        self.op("sp", lambda e: e.nop(), reads=keys)

    def emit(self):
        nc = self.nc

        def skip(d, o, kind):
            return (not d.is_dma) and d.eng == o.eng and (not o.is_dma) and o.eng == "pe"

        for o in self.all_ops:
            for d, kind in o.deps:
                if d.is_dma or skip(d, o, kind):
                    continue
                d.marked = True
        for e in self.ENGS:
            c = 0
            for o in self.q[e]:
                if o.is_dma:
                    continue
                if o.marked:
                    c += 1
                    o.count = c
                    o.sem = "eng_" + e
        sems = {}
        for e in self.ENGS:
            sems["eng_" + e] = self.stack.enter_context(nc.semaphore("s_eng_" + e))
        for slot in self.dma_slots:
            sems[slot] = self.stack.enter_context(nc.semaphore("s_" + str(slot)))

        def run(e, eng):
            waited = {}
            for o in self.q[e]:
                need = {}
                for d, kind in o.deps:
                    if skip(d, o, kind):
                        continue
                    if need.get(d.sem, 0) < d.count:
                        need[d.sem] = d.count
                for s, c in need.items():
                    if waited.get(s, 0) < c:
                        eng.wait_ge(sems[s], c)
                        waited[s] = c
                ins = o.fn(eng)
                if o.marked:
                    ins.then_inc(sems[o.sem], 16 if o.is_dma else 1)

        with nc.Block() as block:
            @block.tensor
            def _(eng):
                run("pe", eng)

            @block.scalar
            def _(eng):
                run("act", eng)

            @block.vector
            def _(eng):
                run("dve", eng)

            @block.gpsimd
            def _(eng):
                run("pool", eng)

            @block.sync
            def _(eng):
                run("sp", eng)
        self.stack.close()


def tgroups(n_tok):
    out = []
    t = 0
    while t < n_tok:
        n = min(512, n_tok - t)
        out.append((t, n))
        t += n
    return out


TGS = tgroups(T)
NG = len(TGS)


def new_nc():
    return bass.Bass("TRN2", target_bir_lowering=False)


def din(nc, name, shape, dt=F32):
    return nc.dram_tensor(name, list(shape), dt, kind="ExternalInput").ap()


def dout(nc, name, shape, dt=F32):
    return nc.dram_tensor(name, list(shape), dt, kind="ExternalOutput").ap()


def TT(P, eng, out, in0, in1, op, reads, writes):
    return P.op(eng, lambda e: e.tensor_tensor(out=out, in0=in0, in1=in1, op=op), reads=reads, writes=writes)


def TS(P, eng, out, in0, s1, s2, op0, op1, reads, writes):
    if s2 is None:
        return P.op(eng, lambda e: e.tensor_scalar(out=out, in0=in0, scalar1=s1, scalar2=None, op0=op0), reads=reads, writes=writes)
    return P.op(eng, lambda e: e.tensor_scalar(out=out, in0=in0, scalar1=s1, scalar2=s2, op0=op0, op1=op1), reads=reads, writes=writes)


def STT(P, out, in0, scalar, in1, op0, op1, reads, writes):
    return P.op("dve", lambda e: e.scalar_tensor_tensor(out=out, in0=in0, scalar=scalar, in1=in1, op0=op0, op1=op1),
                reads=reads, writes=writes)


def ACT(P, out, in_, func, reads, writes, bias=None, scale=1.0):
    if bias is None:
        return P.op("act", lambda e: e.activation(out=out, in_=in_, func=func, scale=scale), reads=reads, writes=writes)
    return P.op("act", lambda e: e.activation(out=out, in_=in_, func=func, bias=bias, scale=scale), reads=reads, writes=writes)


class LNParts:
    def __init__(self, P, nD, name="ln"):
        self.P = P
        self.sq = [P.sbuf(f"{name}_sq{i}", [128, 512], F32) for i in range(2)]
        self.mean = P.sbuf(f"{name}_mean", [128, 512], F32)
        self.msq = P.sbuf(f"{name}_msq", [128, 512], F32)
        self.rstd = P.sbuf(f"{name}_rstd", [128, 512], F32)
        self.t1 = [P.sbuf(f"{name}_t1{i}", [128, 512], F32) for i in range(2)]
        self.t2 = [P.sbuf(f"{name}_t2{i}", [128, 512], F32) for i in range(2)]
        self.ones = P.sbuf(f"{name}_ones", [128, 128], F32)
        self.epsb = P.sbuf(f"{name}_epsb", [128, 1], F32)
        P.op("pool", lambda e: e.memset(self.ones[:], 1.0 / nD), writes=["ln_ones"])
        P.op("pool", lambda e: e.memset(self.epsb[:], LN_EPS), writes=["ln_epsb"])


def layer_norm(P, LP, r, hb, vt, gk, bk, ps0, ps1, psk0, psk1, out_scale_tiles=None):
    for g, (t0, n) in enumerate(TGS):
        for dt in range(DT):
            sqi = P.rot("ln_sq")
            sq = LP.sq[sqi]
            ACT(P, sq[:, :n], r[:, dt, t0:t0 + n], AF.Square, [("r", dt, g)], [("ln_sq", sqi)])
            P.op("pe", lambda e, dt=dt, t0=t0, n=n: e.matmul(ps0[:, :n], LP.ones[:], r[:, dt, t0:t0 + n], start=(dt == 0), stop=(dt == DT - 1)),
                 reads=["ln_ones", ("r", dt, g)], writes=[psk0])
            P.op("pe", lambda e, sq=sq, dt=dt, n=n: e.matmul(ps1[:, :n], LP.ones[:], sq[:, :n], start=(dt == 0), stop=(dt == DT - 1)),
                 reads=["ln_ones", ("ln_sq", sqi)], writes=[psk1])
        mean, msq, rstd = LP.mean, LP.msq, LP.rstd
        P.op("dve", lambda e, n=n: e.tensor_copy(out=mean[:, :n], in_=ps0[:, :n]), reads=[psk0], writes=["ln_mean"])
        TT(P, "dve", msq[:, :n], mean[:, :n], mean[:, :n], ALU.mult, ["ln_mean"], ["ln_msq"])
        TT(P, "dve", msq[:, :n], ps1[:, :n], msq[:, :n], ALU.subtract, [psk1, "ln_msq"], ["ln_msq"])
        ACT(P, msq[:, :n], msq[:, :n], AF.Sqrt, ["ln_msq", "ln_epsb"], ["ln_msq"], bias=LP.epsb[:])
        P.op("dve", lambda e, n=n: e.reciprocal(out=rstd[:, :n], in_=msq[:, :n]), reads=["ln_msq"], writes=["ln_rstd"])
        for dt in range(DT):
            i1 = P.rot("ln_t1")
            t1 = LP.t1[i1]
            i2 = P.rot("ln_t2")
            t2 = LP.t2[i2]
            TT(P, "pool", t1[:, :n], r[:, dt, t0:t0 + n], mean[:, :n], ALU.subtract, [("r", dt, g), "ln_mean"], [("ln_t1", i1)])
            TT(P, "dve", t2[:, :n], t1[:, :n], rstd[:, :n], ALU.mult, [("ln_t1", i1), "ln_rstd"], [("ln_t2", i2)])
            if hb is not None:
                ACT(P, hb[:, dt, t0:t0 + n], t2[:, :n], AF.Identity, [("ln_t2", i2), "vecs"], [("hb", dt, g)],
                    bias=vt[:, bk + dt:bk + dt + 1], scale=vt[:, gk + dt:gk + dt + 1])
            TS(P, "pool", r[:, dt, t0:t0 + n], t2[:, :n], vt[:, gk + dt:gk + dt + 1], vt[:, bk + dt:bk + dt + 1], ALU.mult, ALU.add,
               [("ln_t2", i2), "vecs", ("r", dt, g)], [("r", dt, g)])


def load_fm(P, dst, src, key, queue="sp", slot="ldfm"):
    for dt in range(DT):
        P.dma(queue, lambda e, dt=dt: e.dma_start(out=dst[:, dt, :], in_=src[dt * 128:(dt + 1) * 128, :]),
              f"{slot}{dt % 4}", writes=[(key, dt, g) for g in range(NG)])


def store_fm(P, dst, src, key, queue="sp", slot="stfm"):
    for dt in range(DT):
        P.dma(queue, lambda e, dt=dt: e.dma_start(out=dst[dt * 128:(dt + 1) * 128, :], in_=src[:, dt, :]),
              f"{slot}{dt % 4}", reads=[(key, dt, g) for g in range(NG)], writes=[("out_" + key, dt)])
    P.finish([("out_" + key, dt) for dt in range(DT)])


def build_ffn():
    nc = new_nc()
    h = din(nc, "h", [D, T])
    w1 = din(nc, "w1", [D, FF])
    w3 = din(nc, "w3", [D, FF])
    w2 = din(nc, "w2", [FF, D])
    vecs = din(nc, "vecs", [128, 2 * DT])
    y = dout(nc, "y", [D, T])
    P = Prog(nc)
    GL = 12
    r = P.sbuf("r", [128, DT, T], F32)
    hb = P.sbuf("hb", [128, DT, T], BF16)
    h1 = P.sbuf("h1", [128, GL, T], BF16)
    w1c = [P.sbuf(f"w1c{i}", [128, DT, 256], BF16) for i in range(2)]
    w3c = [P.sbuf(f"w3c{i}", [128, DT, 256], BF16) for i in range(2)]
    w2c = [P.sbuf(f"w2c{i}", [128, GL, 256], BF16) for i in range(2)]
    sgs = [P.sbuf(f"sg{i}", [128, 512], F32) for i in range(2)]
    vt = P.sbuf("vt", [128, 2 * DT], F32)
    LP = LNParts(P, D)
    psA = [P.psum(f"psA{i}", [128, 512]) for i in range(2)]
    psB = [P.psum(f"psB{i}", [128, 512]) for i in range(2)]
    psO = [P.psum(f"psO{i}", [128, 512]) for i in range(2)]
    psS = [P.psum(f"psS{i}", [128, 512]) for i in range(2)]
    P.dma("sp", lambda e: e.dma_start(out=vt[:], in_=vecs), "ldv", writes=["vecs"])
    load_fm(P, r, h, "r")
    for dt in range(DT):
        for g, (t0, n) in enumerate(TGS):
            ACT(P, hb[:, dt, t0:t0 + n], r[:, dt, t0:t0 + n], AF.Copy, [("r", dt, g)], [("hb", dt, g)])
            TS(P, "dve", r[:, dt, t0:t0 + n], r[:, dt, t0:t0 + n], float(ALPHA), None, ALU.mult, None,
               [("r", dt, g), ("hb", dt, g)], [("r", dt, g)])
    f0 = 0
    while f0 < FT:
        gl = min(GL, FT - f0)
        for c in range(gl // 2):
            ft0 = f0 + 2 * c
            s = P.rot("w13")
            a1, a3 = w1c[s], w3c[s]
            P.dma("pool", lambda e, a1=a1, ft0=ft0: e.dma_start(
                out=a1[:], in_=w1[:, ft0 * 128:(ft0 + 2) * 128].rearrange("(kt p) f -> p kt f", p=128)),
                f"w1c{s}", writes=[("w1c", s)])
            P.dma("pool", lambda e, a3=a3, ft0=ft0: e.dma_start(
                out=a3[:], in_=w3[:, ft0 * 128:(ft0 + 2) * 128].rearrange("(kt p) f -> p kt f", p=128)),
                f"w3c{s}", writes=[("w3c", s)])
            for j in range(2):
                fl = 2 * c + j
                for g, (t0, n) in enumerate(TGS):
                    pa = P.rot("psA")
                    pA, pB = psA[pa], psB[pa]

                    def mm(e, w=a1, j=j, t0=t0, n=n, ps=pA):
                        for kt in range(DT):
                            ins = e.matmul(ps[:, :n], w[:, kt, j * 128:(j + 1) * 128], hb[:, kt, t0:t0 + n],
                                           start=(kt == 0), stop=(kt == DT - 1))
                        return ins
                    hbk = [("hb", kt, g) for kt in range(DT)]
                    P.op("pe", mm, reads=[("w1c", s)] + hbk, writes=[("psA", pa)])
                    P.op("pe", lambda e, mm=mm, a3=a3, pB=pB: mm(e, w=a3, ps=pB), reads=[("w3c", s)] + hbk, writes=[("psB", pa)])
                    sgi = P.rot("sg")
                    sg = sgs[sgi]
                    ACT(P, sg[:, :n], pA[:, :n], AF.Silu, [("psA", pa)], [("sg", sgi)])
                    TT(P, "dve", h1[:, fl, t0:t0 + n], pB[:, :n], sg[:, :n], ALU.mult, [("psB", pa), ("sg", sgi)], [("h1", fl, g)])
        for dc in range(DT // 2):
            s = P.rot("w2")
            a2 = w2c[s]
            P.dma("pool", lambda e, a2=a2, f0=f0, gl=gl, dc=dc: e.dma_start(
                out=a2[:, :gl, :], in_=w2[f0 * 128:(f0 + gl) * 128, dc * 256:(dc + 1) * 256].rearrange("(fl p) d -> p fl d", p=128)),
                f"w2c{s}", writes=[("w2c", s)])
            for j in range(2):
                dt = 2 * dc + j
                for g, (t0, n) in enumerate(TGS):
                    po = P.rot("psO")
                    pO = psO[po]

                    def mmO(e, a2=a2, j=j, t0=t0, n=n, pO=pO, gl=gl):
                        for fl in range(gl):
                            ins = e.matmul(pO[:, :n], a2[:, fl, j * 128:(j + 1) * 128], h1[:, fl, t0:t0 + n],
                                           start=(fl == 0), stop=(fl == gl - 1))
                        return ins
                    P.op("pe", mmO, reads=[("w2c", s)] + [("h1", fl, g) for fl in range(gl)], writes=[("psO", po)])
                    STT(P, r[:, dt, t0:t0 + n], pO[:, :n], 0.5, r[:, dt, t0:t0 + n], ALU.mult, ALU.add,
                        [("psO", po), ("r", dt, g)], [("r", dt, g)])
        f0 += gl
    layer_norm(P, LP, r, None, vt, 0, DT, psS[0], psS[1], ("psS", 0), ("psS", 1))
    store_fm(P, y, r, "r")
    P.emit()
    return nc


def build_linear(n_out, with_f):
    nc = new_nc()
    h = din(nc, "h", [D, T])
    w = din(nc, "w", [D, n_out + (FX_H if with_f else 0)])
    y = dout(nc, "y", [n_out, T], BF16)
    if with_f:
        bf = din(nc, "bf", [FX_H, 1])
        sp_out = dout(nc, "sp", [FX_H, T])
    P = Prog(nc)
    r = P.sbuf("r", [128, DT, T], F32)
    hb = P.sbuf("hb", [128, DT, T], BF16)
    wc = [P.sbuf(f"wc{i}", [128, DT, 256], BF16) for i in range(2)]
    ob = [P.sbuf(f"ob{i}", [128, T], BF16) for i in range(4)]
    ps = [P.psum(f"ps{i}", [128, 512]) for i in range(4)]
    load_fm(P, r, h, "r")
    for dt in range(DT):
        for g, (t0, n) in enumerate(TGS):
            ACT(P, hb[:, dt, t0:t0 + n], r[:, dt, t0:t0 + n], AF.Copy, [("r", dt, g)], [("hb", dt, g)])
    outk = []
    for c in range(n_out // 256):
        s = P.rot("wc")
        wt = wc[s]
        P.dma("pool", lambda e, wt=wt, c=c: e.dma_start(
            out=wt[:], in_=w[:, c * 256:(c + 1) * 256].rearrange("(kt p) f -> p kt f", p=128)),
            f"wc{s}", writes=[("wc", s)])
        for j in range(2):
            nt = 2 * c + j
            oi = P.rot("ob", 4)
            o = ob[oi]
            for g, (t0, n) in enumerate(TGS):
                pi = P.rot("ps", 4)
                pp = ps[pi]

                def mm(e, wt=wt, j=j, t0=t0, n=n, pp=pp):
                    for kt in range(DT):
                        ins = e.matmul(pp[:, :n], wt[:, kt, j * 128:(j + 1) * 128], hb[:, kt, t0:t0 + n],
                                       start=(kt == 0), stop=(kt == DT - 1))
                    return ins
                P.op("pe", mm, reads=[("wc", s)] + [("hb", kt, g) for kt in range(DT)], writes=[("ps", pi)])
                if g % 2 == 0:
                    ACT(P, o[:, t0:t0 + n], pp[:, :n], AF.Copy, [("ps", pi)], [("ob", oi, g)])
                else:
                    P.op("dve", lambda e, o=o, pp=pp, t0=t0, n=n: e.tensor_copy(out=o[:, t0:t0 + n], in_=pp[:, :n]),
                         reads=[("ps", pi)], writes=[("ob", oi, g)])
            P.dma("sp", lambda e, o=o, nt=nt: e.dma_start(out=y[nt * 128:(nt + 1) * 128, :], in_=o[:]), f"sty{oi}",
                  reads=[("ob", oi, g) for g in range(NG)], writes=[("y", nt)])
            outk.append(("y", nt))
    if with_f:
        wf = P.sbuf("wf", [128, DT, FX_H], BF16)
        bft = P.sbuf("bft", [FX_H, 1], F32)
        nbf = P.sbuf("nbf", [FX_H, 1], F32)
        spb = P.sbuf("spb", [FX_H, T], F32)
        ex = P.sbuf("ex", [FX_H, 512], F32)
        P.dma("pool", lambda e: e.dma_start(out=wf[:], in_=w[:, n_out:n_out + FX_H].rearrange("(kt p) f -> p kt f", p=128)),
              "wf", writes=["wf"])
        P.dma("sp", lambda e: e.dma_start(out=bft[:], in_=bf), "bf", writes=["bf"])
        TS(P, "dve", nbf[:], bft[:], -1.0, None, ALU.mult, None, ["bf"], ["nbf"])
        for g, (t0, n) in enumerate(TGS):
            pi = P.rot("ps", 4)
            pp = ps[pi]

            def mmf(e, t0=t0, n=n, pp=pp):
                for kt in range(DT):
                    ins = e.matmul(pp[:FX_H, :n], wf[:, kt, :], hb[:, kt, t0:t0 + n], start=(kt == 0), stop=(kt == DT - 1))
                return ins
            P.op("pe", mmf, reads=["wf"] + [("hb", kt, g) for kt in range(DT)], writes=[("ps", pi)])
            ACT(P, ex[:, :n], pp[:FX_H, :n], AF.Exp, [("ps", pi), "nbf"], ["ex"], bias=nbf[:], scale=-1.0)
            TS(P, "dve", ex[:, :n], ex[:, :n], 1.0, None, ALU.add, None, ["ex"], ["ex"])
            ACT(P, spb[:, t0:t0 + n], ex[:, :n], AF.Ln, ["ex"], [("spb", g)])
        P.dma("sp", lambda e: e.dma_start(out=sp_out, in_=spb[:]), "stsp", reads=[("spb", g) for g in range(NG)], writes=["spo"])
        outk.append("spo")
    P.finish(outk)
    P.emit()
    return nc


def build_post(rwkv):
    nc = new_nc()
    h = din(nc, "h", [D, T])
    wo = din(nc, "wo", [D, D])
    vecs = din(nc, "vecs", [128, 4 * DT])
    y = dout(nc, "y", [D, T])
    if rwkv:
        ys = din(nc, "ys", [D, T])
        bon = din(nc, "bon", [D, T])
        gg = din(nc, "gg", [D, T])
        cst = din(nc, "cst", [128, 128])
    else:
        u = din(nc, "u", [D, T], BF16)
    P = Prog(nc)
    r = P.sbuf("r", [128, DT, T], F32)
    ub = P.sbuf("ub", [128, DT, T], BF16)
    wc = [P.sbuf(f"wc{i}", [128, DT, 256], BF16) for i in range(2)]
    vt = P.sbuf("vt", [128, 4 * DT], F32)
    LP = LNParts(P, D)
    ps = [P.psum(f"ps{i}", [128, 512]) for i in range(4)]
    psS = [P.psum(f"psS{i}", [128, 512]) for i in range(2)]
    P.dma("sp", lambda e: e.dma_start(out=vt[:], in_=vecs), "ldv", writes=["vecs"])
    load_fm(P, r, h, "r")
    for dt in range(DT):
        for g, (t0, n) in enumerate(TGS):
            TS(P, "pool", r[:, dt, t0:t0 + n], r[:, dt, t0:t0 + n], float(ALPHA), None, ALU.mult, None, [("r", dt, g)], [("r", dt, g)])
    if not rwkv:
        load_fm(P, ub, u, "ub", slot="ldu")
    else:
        bo = P.sbuf("bo", [128, 128], F32)
        gnb = P.sbuf("gnb", [128, 1], F32)
        P.dma("sp", lambda e: e.dma_start(out=bo[:], in_=cst), "ldc", writes=["bo"])
        P.op("pool", lambda e: e.memset(gnb[:], GN_EPS), writes=["gnb"])
        yb = [P.sbuf(f"yb{i}", [128, T], F32) for i in range(2)]
        bb = [P.sbuf(f"bb{i}", [128, T], F32) for i in range(2)]
        gb = [P.sbuf(f"gb{i}", [128, T], F32) for i in range(2)]
        sq = P.sbuf("gsq", [128, 512], F32)
        mean = P.sbuf("gmean", [128, 512], F32)
        var = P.sbuf("gvar", [128, 512], F32)
        tmp = P.sbuf("gtmp", [128, 512], F32)
        for dt in range(DT):
            s = P.rot("yb")
            P.dma("sp", lambda e, s=s, dt=dt: e.dma_start(out=yb[s][:], in_=ys[dt * 128:(dt + 1) * 128, :]), f"ldy{s}", writes=[("yb", s)])
            P.dma("sp", lambda e, s=s, dt=dt: e.dma_start(out=bb[s][:], in_=bon[dt * 128:(dt + 1) * 128, :]), f"ldb{s}", writes=[("bb", s)])
            P.dma("sp", lambda e, s=s, dt=dt: e.dma_start(out=gb[s][:], in_=gg[dt * 128:(dt + 1) * 128, :]), f"ldg{s}", writes=[("gb", s)])
            for g, (t0, n) in enumerate(TGS):
                yv = yb[s][:, t0:t0 + n]
                ACT(P, sq[:, :n], yv, AF.Square, [("yb", s)], ["gsq"])
                P.op("pe", lambda e, yv=yv, n=n: e.matmul(psS[0][:, :n], bo[:], yv, start=True, stop=True), reads=["bo", ("yb", s)], writes=[("psS", 0)])
                P.op("pe", lambda e, n=n: e.matmul(psS[1][:, :n], bo[:], sq[:, :n], start=True, stop=True), reads=["bo", "gsq"], writes=[("psS", 1)])
                P.op("dve", lambda e, n=n: e.tensor_copy(out=mean[:, :n], in_=psS[0][:, :n]), reads=[("psS", 0)], writes=["gmean"])
                TT(P, "dve", var[:, :n], mean[:, :n], mean[:, :n], ALU.mult, ["gmean"], ["gvar"])
                TT(P, "dve", var[:, :n], psS[1][:, :n], var[:, :n], ALU.subtract, [("psS", 1), "gvar"], ["gvar"])
                ACT(P, var[:, :n], var[:, :n], AF.Sqrt, ["gvar", "gnb"], ["gvar"], bias=gnb[:])
                P.op("dve", lambda e, n=n: e.reciprocal(out=var[:, :n], in_=var[:, :n]), reads=["gvar"], writes=["gvar"])
                TT(P, "pool", tmp[:, :n], yv, mean[:, :n], ALU.subtract, [("yb", s), "gmean"], ["gtmp"])
                TT(P, "dve", tmp[:, :n], tmp[:, :n], var[:, :n], ALU.mult, ["gtmp", "gvar"], ["gtmp"])
                TS(P, "pool", tmp[:, :n], tmp[:, :n], vt[:, 2 * DT + dt:2 * DT + dt + 1], vt[:, 3 * DT + dt:3 * DT + dt + 1], ALU.mult, ALU.add,
                   ["gtmp", "vecs"], ["gtmp"])
                TT(P, "pool", tmp[:, :n], tmp[:, :n], bb[s][:, t0:t0 + n], ALU.add, ["gtmp", ("bb", s)], ["gtmp"])
                TT(P, "dve", ub[:, dt, t0:t0 + n], tmp[:, :n], gb[s][:, t0:t0 + n], ALU.mult, ["gtmp", ("gb", s)], [("ub", dt, g)])
    for c in range(DT // 2):
        s = P.rot("wc")
        wt = wc[s]
        P.dma("pool", lambda e, wt=wt, c=c: e.dma_start(
            out=wt[:], in_=wo[:, c * 256:(c + 1) * 256].rearrange("(kt p) f -> p kt f", p=128)),
            f"wc{s}", writes=[("wc", s)])
        for j in range(2):
            dt = 2 * c + j
            for g, (t0, n) in enumerate(TGS):
                pi = P.rot("ps", 4)
                pp = ps[pi]

                def mm(e, wt=wt, j=j, t0=t0, n=n, pp=pp):
                    for kt in range(DT):
                        ins = e.matmul(pp[:, :n], wt[:, kt, j * 128:(j + 1) * 128], ub[:, kt, t0:t0 + n],
                                       start=(kt == 0), stop=(kt == DT - 1))
                    return ins
                P.op("pe", mm, reads=[("wc", s)] + [("ub", kt, g) for kt in range(DT)], writes=[("ps", pi)])
                STT(P, r[:, dt, t0:t0 + n], pp[:, :n], 1.0, r[:, dt, t0:t0 + n], ALU.mult, ALU.add,
                    [("ps", pi), ("r", dt, g)], [("r", dt, g)])
    layer_norm(P, LP, r, None, vt, 0, DT, psS[0], psS[1], ("psS", 0), ("psS", 1))
    store_fm(P, y, r, "r")
    P.emit()
    return nc


_CACHE = {}


def get_nc(key, builder, *args):
    if key not in _CACHE:
        _CACHE[key] = builder(*args)
    return _CACHE[key]


def run(nc, in_maps):
    res = run_bass_kernel_spmd(nc, in_maps, core_ids=list(range(NC)))
    return res.results


def to_fm(h):
    flat = h.reshape(B * L, h.shape[-1])
    return [np.ascontiguousarray(flat[c * T:(c + 1) * T].T) for c in range(NC)]


def from_fm(shards):
    flat = np.concatenate([s.T for s in shards], axis=0)
    return flat.reshape(B, L, flat.shape[-1])


def pv(v):
    v = np.asarray(v).reshape(-1, 128)
    return np.ascontiguousarray(v.T)


def stage_ffn(h_sh, w1, w3, w2, g, b):
    nc = get_nc("ffn", build_ffn)
    vecs = np.ascontiguousarray(np.concatenate([pv(g), pv(b)], axis=1))
    res = run(nc, [{"h": h_sh[c], "w1": w1, "w3": w3, "w2": w2, "vecs": vecs} for c in range(NC)])
    return [r["y"] for r in res]


NHA = 8
NTL = (L + 127) // 128
LASTN = L - (NTL - 1) * 128


def tile_n(i):
    return 128 if i < NTL - 1 else LASTN


def build_attn():
    nc = new_nc()
    qT = din(nc, "qT", [NHA, 128, L], BF16)
    kT = din(nc, "kT", [NHA, 128, L], BF16)
    v = din(nc, "v", [NHA, L, 128], BF16)
    sp = din(nc, "sp", [NHA, L])
    cst = din(nc, "cst", [128, 4 * 128])
    o = dout(nc, "o", [NHA, L, 128], BF16)
    P = Prog(nc)
    q_sb = P.sbuf("q_sb", [128, NHA, L], BF16)
    k_sb = P.sbuf("k_sb", [128, NHA, L], BF16)
    va = P.sbuf("va", [128, NHA, NTL, 132], BF16)
    ct = P.sbuf("ct", [128, 4 * 128], F32)
    spt = P.sbuf("spt", [NHA, L], F32)
    onesr = P.sbuf("onesr", [NHA, L], F32)
    cs = P.sbuf("cs", [NHA, L], F32)
    csT = P.sbuf("csT", [128, NTL * NHA], F32)
    csref = P.sbuf("csref", [128, NTL * NHA], F32)
    BI = P.sbuf("BI", [128, NHA, NTL, NTL], F32)
    pT = [P.sbuf(f"pT{i}", [128, NTL, 128], BF16) for i in range(2)]
    ob = [P.sbuf(f"ob{i}", [128, NTL, 128], BF16) for i in range(2)]
    rc = [P.sbuf(f"rc{i}", [128, 1], F32) for i in range(2)]
    psS = [P.psum(f"psS{i}", [128, 4, 128]) for i in range(4)]
    psO = [P.psum(f"psO{i}", [128, 512]) for i in range(2)]
    psT = P.psum("psT", [128, 512])
    psR = P.psum("psR", [128, 512])
    ident = ct[:, 0:128]
    maskc = ct[:, 128:256]
    sel127 = ct[:, 256:384]
    sel15 = ct[:, 384:512]
    P.dma("sp", lambda e: e.dma_start(out=ct[:], in_=cst), "ldc", writes=["ct"])
    P.dma("sp", lambda e: e.dma_start(out=spt[:], in_=sp), "ldsp", writes=["spt"])
    P.op("pool", lambda e: e.memset(va[:], 1.0), writes=["va"])
    P.op("pool", lambda e: e.memset(onesr[:], 1.0), writes=["onesr"])
    P.op("pool", lambda e: e.memset(csT[:], 0.0), writes=["csT"])
    for h in range(NHA):
        P.dma("sp", lambda e, h=h: e.dma_start(out=q_sb[:, h, :], in_=qT[h]), f"ldq{h % 2}", writes=[("q", h)])
        P.dma("sp", lambda e, h=h: e.dma_start(out=k_sb[:, h, :], in_=kT[h]), f"ldk{h % 2}", writes=[("k", h)])
        P.dma("sp", lambda e, h=h: e.dma_start(out=va[:, h, 0:NTL - 1, 0:128],
                                                in_=v[h, 0:(NTL - 1) * 128, :].rearrange("(si p) d -> p si d", p=128)),
              f"ldv{h % 2}", reads=["va"], writes=[("va", h)])
        P.dma("sp", lambda e, h=h: e.dma_start(out=va[0:LASTN, h, NTL - 1, 0:128], in_=v[h, (NTL - 1) * 128:L, :]),
              f"ldv{h % 2}", reads=["va"], writes=[("va2", h)])
    P.op("dve", lambda e: e.tensor_tensor_scan(out=cs[:], data0=onesr[:], data1=spt[:], initial=0.0, op0=ALU.mult, op1=ALU.add),
         reads=["onesr", "spt"], writes=["cs"])
    for si in range(NTL):
        n = tile_n(si)
        P.op("pe", lambda e, si=si, n=n: e.transpose(psT[:n, si * NHA:(si + 1) * NHA], cs[:, si * 128:si * 128 + n], ct[0:NHA, 0:NHA]),
             reads=["cs", "ct"], writes=["psT"])
    P.op("dve", lambda e: e.tensor_copy(out=csT[:, 0:(NTL - 1) * NHA], in_=psT[:, 0:(NTL - 1) * NHA]), reads=["psT", "csT"], writes=["csT"])
    P.op("dve", lambda e: e.tensor_copy(out=csT[0:LASTN, (NTL - 1) * NHA:NTL * NHA], in_=psT[0:LASTN, (NTL - 1) * NHA:NTL * NHA]),
         reads=["psT", "csT"], writes=["csT"])
    P.op("pe", lambda e: e.matmul(psR[:, 0:(NTL - 1) * NHA], sel127, csT[:, 0:(NTL - 1) * NHA], start=True, stop=True),
         reads=["ct", "csT"], writes=["psR"])
    P.op("pe", lambda e: e.matmul(psR[:, (NTL - 1) * NHA:NTL * NHA], sel15, csT[:, (NTL - 1) * NHA:NTL * NHA], start=True, stop=True),
         reads=["ct", "csT"], writes=["psR"])
    P.op("dve", lambda e: e.tensor_copy(out=csref[:], in_=psR[:, 0:NTL * NHA]), reads=["psR"], writes=["csref"])
    csT3 = csT[:].rearrange("p (si h) -> p si h", h=NHA)
    for h in range(NHA):
        for qi in range(NTL):
            TS(P, "pool", BI[:, h, qi, :], csT3[:, :, h], csref[:, qi * NHA + h:qi * NHA + h + 1], None, ALU.subtract, None,
               ["csT", "csref"], [("BI", h)])
    scale = float(FX_N ** -0.5)
    outk = []
    for h in range(NHA):
        obi = P.rot("ob")
        obt = ob[obi]
        for qi in range(NTL):
            nq = tile_n(qi)
            pi = P.rot("pT")
            pt = pT[pi]
            for s0 in range(0, qi + 1, 4):
                sis = list(range(s0, min(s0 + 4, qi + 1)))
                bank = P.rot("psS", 4)

                def mms(e, bank=bank, sis=sis, h=h, qi=qi, nq=nq):
                    for slot, si in enumerate(sis):
                        ns = tile_n(si)
                        ins = e.matmul(psS[bank][:ns, slot, :nq], k_sb[:, h, si * 128:si * 128 + ns],
                                       q_sb[:, h, qi * 128:qi * 128 + nq], start=True, stop=True)
                    return ins
                P.op("pe", mms, reads=[("k", h), ("q", h)], writes=[("psS", bank)])
                for slot, si in enumerate(sis):
                    ns = tile_n(si)
                    ACT(P, pt[:ns, si, :nq], psS[bank][:ns, slot, :nq], AF.Exp, [("psS", bank), ("BI", h)], [("pT", pi, si)],
                        bias=BI[:ns, h, qi, si:si + 1], scale=scale)
                    if si == qi:
                        TT(P, "pool", pt[:ns, si, :nq], pt[:ns, si, :nq], maskc[:ns, :nq], ALU.mult, [("pT", pi, si), "ct"], [("pT", pi, si)])
            oi = P.rot("psO")
            po = psO[oi]

            def mmo(e, pt=pt, po=po, h=h, qi=qi, nq=nq):
                for si in range(qi + 1):
                    ns = tile_n(si)
                    ins = e.matmul(po[:nq, 0:129], pt[:ns, si, :nq], va[:ns, h, si, 0:129], start=(si == 0), stop=(si == qi))
                return ins
            P.op("pe", mmo, reads=[("pT", pi, si) for si in range(qi + 1)] + [("va", h), ("va2", h)], writes=[("psO", oi)])
            ri = P.rot("rc")
            P.op("dve", lambda e, ri=ri, po=po, nq=nq: e.reciprocal(out=rc[ri][:nq, :], in_=po[:nq, 128:129]), reads=[("psO", oi)], writes=[("rc", ri)])
            TS(P, "dve", obt[:nq, qi, :], po[:nq, 0:128], rc[ri][:nq, 0:1], None, ALU.mult, None, [("psO", oi), ("rc", ri)], [("ob", obi, qi)])
        P.dma("sp", lambda e, obt=obt, h=h: e.dma_start(out=o[h, 0:(NTL - 1) * 128, :].rearrange("(si p) d -> p si d", p=128),
                                                        in_=obt[:, 0:NTL - 1, :]), f"sto{obi}",
              reads=[("ob", obi, qi) for qi in range(NTL - 1)], writes=[("o", h, 0)])
        P.dma("sp", lambda e, obt=obt, h=h: e.dma_start(out=o[h, (NTL - 1) * 128:L, :], in_=obt[0:LASTN, NTL - 1, :]), f"sto{obi}",
              reads=[("ob", obi, NTL - 1)], writes=[("o", h, 1)])
        outk += [("o", h, 0), ("o", h, 1)]
    P.finish(outk)
    P.emit()
    return nc


def attn_consts():
    c = np.zeros((128, 512), np.float32)
    c[:, 0:128] = np.eye(128, dtype=np.float32)
    c[:, 128:256] = np.triu(np.ones((128, 128), np.float32))
    c[127, 256:384] = 1.0
    c[15, 384:512] = 1.0
    return c


def stage_linear(h_sh, w, n_out, bf=None):
    with_f = bf is not None
    nc = get_nc(("lin", n_out, with_f), build_linear, n_out, with_f)
    maps = []
    for c in range(NC):
        m = {"h": h_sh[c], "w": w}
        if with_f:
            m["bf"] = np.ascontiguousarray(bf.reshape(FX_H, 1))
        maps.append(m)
    res = run(nc, maps)
    if with_f:
        return [r["y"] for r in res], [r["sp"] for r in res]
    return [r["y"] for r in res]


def fm_to_heads(sh, nh, hd):
    full = np.stack([np.concatenate([sh[2 * b], sh[2 * b + 1]], axis=1) for b in range(B)])
    return full.reshape(B, nh, hd, L)


def stage_attn(q_sh, kv_sh, sp_sh):
    nc = get_nc("attn", build_attn)
    q = fm_to_heads(q_sh, FX_H, FX_N)
    k = fm_to_heads([s[0:D] for s in kv_sh], FX_H, FX_N)
    v = fm_to_heads([s[D:2 * D] for s in kv_sh], FX_H, FX_N)
    spf = np.stack([np.concatenate([sp_sh[2 * b], sp_sh[2 * b + 1]], axis=1) for b in range(B)])
    cst = attn_consts()
    maps = []
    for c in range(NC):
        b, hh = c // 2, (c % 2) * NHA
        maps.append({"qT": np.ascontiguousarray(q[b, hh:hh + NHA]), "kT": np.ascontiguousarray(k[b, hh:hh + NHA]),
                     "v": np.ascontiguousarray(v[b, hh:hh + NHA].transpose(0, 2, 1)),
                     "sp": np.ascontiguousarray(spf[b, hh:hh + NHA]), "cst": cst})
    res = run(nc, maps)
    o = np.zeros((B, L, D), dtype=NPBF)
    for c in range(NC):
        b, hh = c // 2, (c % 2) * NHA
        o[b, :, hh * FX_N:(hh + NHA) * FX_N] = res[c]["o"].transpose(1, 0, 2).reshape(L, NHA * FX_N)
    return to_fm(o)


def stage_post(h_sh, u_sh, wo, g, b):
    nc = get_nc("post_fox", build_post, False)
    z = np.zeros_like(pv(g))
    vecs = np.ascontiguousarray(np.concatenate([pv(g), pv(b), z, z], axis=1))
    res = run(nc, [{"h": h_sh[c], "u": u_sh[c], "wo": wo, "vecs": vecs} for c in range(NC)])
    return [r["y"] for r in res]


def emit_mix(P, xdst, hsrc_fn, mu_ap_fn, om_ap_fn, tmpt, dts=range(DT)):
    for dt in dts:
        hs, hkey = hsrc_fn(dt)
        for g, (t0, n) in enumerate(TGS):
            ti = P.rot("mixtmp")
            tm = tmpt[ti]
            TS(P, "pool", tm[:, :n], hs[:, t0:t0 + n], mu_ap_fn(dt), None, ALU.mult, None, [hkey, "vecs"], [("mixtmp", ti)])
            STT(P, xdst[:, dt, t0:t0 + n], hs[:, t0 + 1:t0 + 1 + n], om_ap_fn(dt), tm[:, :n], ALU.mult, ALU.add,
                [hkey, "vecs", ("mixtmp", ti)], [(id(xdst), dt, g)])


def build_rwkv_lora():
    nc = new_nc()
    h = din(nc, "h", [D, T])
    halo = din(nc, "halo", [D, 1])
    vecs = din(nc, "vecs", [128, 3 * DT])
    w1 = din(nc, "w1", [D, 96])
    a1 = din(nc, "a1", [D, 96])
    g1 = din(nc, "g1", [D, 256])
    g2 = din(nc, "g2", [256, D])
    tw_o = dout(nc, "tw", [96, T], BF16)
    ta_o = dout(nc, "ta", [96, T], BF16)
    gg_o = dout(nc, "gg", [D, T])
    P = Prog(nc)
    hh = P.sbuf("hh", [128, DT, T + 1], F32)
    xb = P.sbuf("xb", [128, DT, T], BF16)
    vt = P.sbuf("vt", [128, 3 * DT], F32)
    om = P.sbuf("om", [128, 3 * DT], F32)
    w1s = P.sbuf("w1s", [128, DT, 96], BF16)
    a1s = P.sbuf("a1s", [128, DT, 96], BF16)
    g1s = P.sbuf("g1s", [128, DT, 256], BF16)
    g2s = P.sbuf("g2s", [128, 2, D], BF16)
    twb = P.sbuf("twb", [96, T], BF16)
    tab = P.sbuf("tab", [96, T], BF16)
    tgb = P.sbuf("tgb", [128, 2, T], BF16)
    tmpt = [P.sbuf(f"mt{i}", [128, 512], F32) for i in range(2)]
    gst = [P.sbuf(f"gst{i}", [128, T], F32) for i in range(2)]
    ps = [P.psum(f"ps{i}", [128, 512]) for i in range(4)]
    P.dma("sp", lambda e: e.dma_start(out=vt[:], in_=vecs), "ldv", writes=["vecs0"])
    TS(P, "dve", om[:], vt[:], -1.0, 1.0, ALU.mult, ALU.add, ["vecs0"], ["vecs"])
    for dt in range(DT):
        P.dma("sp", lambda e, dt=dt: e.dma_start(out=hh[:, dt, 1:T + 1], in_=h[dt * 128:(dt + 1) * 128, :]), f"ldh{dt % 4}", writes=[("hh", dt)])
        P.dma("sp", lambda e, dt=dt: e.dma_start(out=hh[:, dt, 0:1], in_=halo[dt * 128:(dt + 1) * 128, :]), f"ldh{dt % 4}", writes=[("hh0", dt)])
    P.dma("pool", lambda e: e.dma_start(out=w1s[:], in_=w1.rearrange("(kt p) f -> p kt f", p=128)), "ldw1", writes=["w1s"])
    P.dma("pool", lambda e: e.dma_start(out=a1s[:], in_=a1.rearrange("(kt p) f -> p kt f", p=128)), "lda1", writes=["a1s"])
    P.dma("pool", lambda e: e.dma_start(out=g1s[:], in_=g1.rearrange("(kt p) f -> p kt f", p=128)), "ldg1", writes=["g1s"])
    P.dma("pool", lambda e: e.dma_start(out=g2s[:], in_=g2.rearrange("(j p) d -> p j d", p=128)), "ldg2", writes=["g2s"])

    class HK:
        pass

    def hsrc(dt):
        return hh[:, dt, :], ("hh", dt)
    xkey = id(xb)

    def mix(i):
        for dt in range(DT):
            for g, (t0, n) in enumerate(TGS):
                ti = P.rot("mixtmp")
                tm = tmpt[ti]
                TS(P, "pool", tm[:, :n], hh[:, dt, t0:t0 + n], vt[:, i * DT + dt:i * DT + dt + 1], None, ALU.mult, None,
                   [("hh", dt), ("hh0", dt), "vecs"], [("mixtmp", ti)])
                STT(P, xb[:, dt, t0:t0 + n], hh[:, dt, t0 + 1:t0 + 1 + n], om[:, i * DT + dt:i * DT + dt + 1], tm[:, :n], ALU.mult, ALU.add,
                    [("hh", dt), "vecs", ("mixtmp", ti)], [("xb", dt, g)])

    def lora_down(ws, wkey, nout, func, dst, dkey):
        for g, (t0, n) in enumerate(TGS):
            pi = P.rot("ps", 4)
            pp = ps[pi]

            def mm(e, t0=t0, n=n, pp=pp):
                for kt in range(DT):
                    ins = e.matmul(pp[:nout, :n], ws[:, kt, 0:nout], xb[:, kt, t0:t0 + n], start=(kt == 0), stop=(kt == DT - 1))
                return ins
            P.op("pe", mm, reads=[wkey] + [("xb", kt, g) for kt in range(DT)], writes=[("ps", pi)])
            ACT(P, dst[:nout, t0:t0 + n], pp[:nout, :n], func, [("ps", pi)], [(dkey, g)])
    mix(0)
    lora_down(w1s, "w1s", 96, AF.Tanh, twb, "twb")
    P.dma("sp", lambda e: e.dma_start(out=tw_o, in_=twb[:]), "sttw", reads=[("twb", g) for g in range(NG)], writes=["tw_o"])
    mix(1)
    lora_down(a1s, "a1s", 96, AF.Copy, tab, "tab")
    P.dma("sp", lambda e: e.dma_start(out=ta_o, in_=tab[:]), "stta", reads=[("tab", g) for g in range(NG)], writes=["ta_o"])
    mix(2)
    for j in range(2):
        for g, (t0, n) in enumerate(TGS):
            pi = P.rot("ps", 4)
            pp = ps[pi]

            def mm(e, j=j, t0=t0, n=n, pp=pp):
                for kt in range(DT):
                    ins = e.matmul(pp[:, :n], g1s[:, kt, j * 128:(j + 1) * 128], xb[:, kt, t0:t0 + n], start=(kt == 0), stop=(kt == DT - 1))
                return ins
            P.op("pe", mm, reads=["g1s"] + [("xb", kt, g) for kt in range(DT)], writes=[("ps", pi)])
            ACT(P, tgb[:, j, t0:t0 + n], pp[:, :n], AF.Sigmoid, [("ps", pi)], [("tgb", j, g)])
    outk = ["tw_o", "ta_o"]
    for dt in range(DT):
        si = P.rot("gst")
        st = gst[si]
        for g, (t0, n) in enumerate(TGS):
            pi = P.rot("ps", 4)
            pp = ps[pi]

            def mm(e, dt=dt, t0=t0, n=n, pp=pp):
                for j in range(2):
                    ins = e.matmul(pp[:, :n], g2s[:, j, dt * 128:(dt + 1) * 128], tgb[:, j, t0:t0 + n], start=(j == 0), stop=(j == 1))
                return ins
            P.op("pe", mm, reads=["g2s"] + [("tgb", j, g) for j in range(2)], writes=[("ps", pi)])
            ACT(P, st[:, t0:t0 + n], pp[:, :n], AF.Copy, [("ps", pi)], [("gst", si, g)])
        P.dma("sp", lambda e, st=st, dt=dt: e.dma_start(out=gg_o[dt * 128:(dt + 1) * 128, :], in_=st[:]), f"stg{si}",
              reads=[("gst", si, g) for g in range(NG)], writes=[("gg_o", dt)])
        outk.append(("gg_o", dt))
    P.finish(outk)
    P.emit()
    return nc


NCH = (T + 127) // 128


def chunks_in(t0, n):
    out = []
    c = 0
    while c < n:
        out.append((t0 + c, min(128, n - c)))
        c += 128
    return out


def build_rwkv_main():
    nc = new_nc()
    h = din(nc, "h", [D, T])
    halo = din(nc, "halo", [D, 1])
    tw_i = din(nc, "tw", [96, T], BF16)
    ta_i = din(nc, "ta", [96, T], BF16)
    wrkv = din(nc, "wrkv", [3, D, D])
    w2 = din(nc, "w2", [96, D])
    a2 = din(nc, "a2", [96, D])
    vecs = din(nc, "vecs", [128, 8 * DT])
    cst = din(nc, "cst", [128, 128])
    names = ["ra", "aa", "bb", "kt", "bh", "kh", "vv"]
    outs = {nm: dout(nc, nm, [D, T], BF16) for nm in names}
    bon_o = dout(nc, "bon", [D, T])
    wc_o = dout(nc, "wc", [128, DT * NCH])
    P = Prog(nc)
    xs = [P.sbuf(f"x{i}", [128, DT, T], BF16) for i in range(3)]
    hbuf = [P.sbuf(f"hbuf{i}", [128, T + 1], F32) for i in range(2)]
    vt = P.sbuf("vt", [128, 8 * DT], F32)
    om = P.sbuf("om", [128, 8 * DT], F32)
    bo = P.sbuf("bo", [128, 128], F32)
    onesb = P.sbuf("onesb", [128, 128], F32)
    twb = P.sbuf("twb", [96, T], BF16)
    tab = P.sbuf("tab", [96, T], BF16)
    w2s = P.sbuf("w2s", [96, D], BF16)
    a2s = P.sbuf("a2s", [96, D], BF16)
    wj = [[P.sbuf(f"wj{i}_{s}", [128, DT, 128], BF16) for i in range(3)] for s in range(2)]
    tmpt = [P.sbuf(f"mt{i}", [128, 512], F32) for i in range(2)]
    tn = ["rj", "vj", "aj", "sw", "kkj", "t1", "k2", "sq", "nrm", "rn", "kkn", "beta", "t2", "cs", "ex", "E1", "E2", "E3", "E4", "bon"]
    tm = {nm: P.sbuf("t_" + nm, [128, 512], F32) for nm in tn}
    so = {nm: P.sbuf("o_" + nm, [128, 512], BF16) for nm in names}
    nb = P.sbuf("nb", [128, NCH], F32)
    wcb = P.sbuf("wcb", [128, DT * NCH], F32)
    pn = ["pR", "pK", "pV", "pW", "pA", "pN", "pB"]
    ps = {nm: P.psum(nm, [128, 512]) for nm in pn}
    P.dma("sp", lambda e: e.dma_start(out=vt[:], in_=vecs), "ldv", writes=["vecs0"])
    TS(P, "dve", om[:], vt[:], -1.0, 1.0, ALU.mult, ALU.add, ["vecs0"], ["vecs"])
    P.dma("sp", lambda e: e.dma_start(out=bo[:], in_=cst), "ldc", writes=["bo"])
    P.op("pool", lambda e: e.memset(onesb[:], 1.0), writes=["onesb"])
    P.dma("sp", lambda e: e.dma_start(out=twb[:], in_=tw_i), "ldtw", writes=["twb"])
    P.dma("sp", lambda e: e.dma_start(out=tab[:], in_=ta_i), "ldta", writes=["tab"])
    P.dma("pool", lambda e: e.dma_start(out=w2s[:], in_=w2), "ldw2", writes=["w2s"])
    P.dma("pool", lambda e: e.dma_start(out=a2s[:], in_=a2), "lda2", writes=["a2s"])
    for dt in range(DT):
        s = P.rot("hbuf")
        hb_ = hbuf[s]
        P.dma("sp", lambda e, hb_=hb_, dt=dt: e.dma_start(out=hb_[:, 1:T + 1], in_=h[dt * 128:(dt + 1) * 128, :]), f"ldh{s}", writes=[("hbuf", s)])
        P.dma("sp", lambda e, hb_=hb_, dt=dt: e.dma_start(out=hb_[:, 0:1], in_=halo[dt * 128:(dt + 1) * 128, :]), f"ldh0{s}", writes=[("hbuf0", s)])
        for i in range(3):
            for g, (t0, n) in enumerate(TGS):
                ti = P.rot("mixtmp")
                tmx = tmpt[ti]
                TS(P, "pool", tmx[:, :n], hb_[:, t0:t0 + n], vt[:, i * DT + dt:i * DT + dt + 1], None, ALU.mult, None,
                   [("hbuf", s), ("hbuf0", s), "vecs"], [("mixtmp", ti)])
                STT(P, xs[i][:, dt, t0:t0 + n], hb_[:, t0 + 1:t0 + 1 + n], om[:, i * DT + dt:i * DT + dt + 1], tmx[:, :n], ALU.mult, ALU.add,
                    [("hbuf", s), "vecs", ("mixtmp", ti)], [("x", i, dt, g)])
    V = lambda k, j: vt[:, k * DT + j:k * DT + j + 1]
    outk = []
    for j in range(DT):
        s = P.rot("wj")
        for i in range(3):
            P.dma("pool", lambda e, s=s, i=i, j=j: e.dma_start(
                out=wj[s][i][:], in_=wrkv[i, :, j * 128:(j + 1) * 128].rearrange("(kt p) f -> p kt f", p=128)),
                f"wj{i}_{s}", writes=[("wj", s, i)])
        for g, (t0, n) in enumerate(TGS):
            def mm16(e, w, x, pp, t0=t0, n=n):
                for kt in range(DT):
                    ins = e.matmul(pp[:, :n], w[:, kt, :], x[:, kt, t0:t0 + n], start=(kt == 0), stop=(kt == DT - 1))
                return ins
            for i, pnm in enumerate(["pR", "pK", "pV"]):
                P.op("pe", lambda e, i=i, pnm=pnm, s=s, mm16=mm16: mm16(e, wj[s][i], xs[i], ps[pnm]),
                     reads=[("wj", s, i)] + [("x", i, kt, g) for kt in range(DT)], writes=[pnm])
            P.op("pe", lambda e, j=j, t0=t0, n=n: e.matmul(ps["pW"][:, :n], w2s[:, j * 128:(j + 1) * 128], twb[:, t0:t0 + n], start=True, stop=True),
                 reads=["w2s", "twb"], writes=["pW"])
            P.op("pe", lambda e, j=j, t0=t0, n=n: e.matmul(ps["pA"][:, :n], a2s[:, j * 128:(j + 1) * 128], tab[:, t0:t0 + n], start=True, stop=True),
                 reads=["a2s", "tab"], writes=["pA"])
            t = {k_: v_[:, :n] for k_, v_ in tm.items()}
            o = {k_: v_[:, :n] for k_, v_ in so.items()}
            p = {k_: v_[:, :n] for k_, v_ in ps.items()}
            ACT(P, t["rj"], p["pR"], AF.Copy, ["pR"], ["rj"])
            ACT(P, t["vj"], p["pV"], AF.Copy, ["pV"], ["vj"])
            ACT(P, t["aj"], p["pA"], AF.Sigmoid, ["pA", "vecs"], ["aj"], bias=V(4, j))
            ACT(P, t["sw"], p["pW"], AF.Sigmoid, ["pW", "vecs"], ["sw"], bias=V(3, j))
            TS(P, "dve", t["kkj"], p["pK"], V(5, j), None, ALU.mult, None, ["pK", "vecs"], ["kkj"])
            TS(P, "pool", t["t1"], t["aj"], V(6, j), om[:, 6 * DT + j:6 * DT + j + 1], ALU.mult, ALU.add, ["aj", "vecs"], ["t1"])
            TT(P, "dve", t["k2"], p["pK"], t["t1"], ALU.mult, ["pK", "t1"], ["k2"])
            TT(P, "pool", t["sq"], t["kkj"], t["kkj"], ALU.mult, ["kkj"], ["sq"])
            P.op("pe", lambda e, t=t, p=p: e.matmul(p["pN"], bo[:], t["sq"], start=True, stop=True), reads=["bo", "sq"], writes=["pN"])
            ACT(P, t["nrm"], p["pN"], AF.Sqrt, ["pN"], ["nrm"])
            TS(P, "dve", t["nrm"], t["nrm"], 1e-12, None, ALU.max, None, ["nrm"], ["nrm"])
            P.op("dve", lambda e, t=t: e.reciprocal(out=t["rn"], in_=t["nrm"]), reads=["nrm"], writes=["rn"])
            TT(P, "pool", t["kkn"], t["kkj"], t["rn"], ALU.mult, ["kkj", "rn"], ["kkn"])
            TT(P, "pool", t["beta"], t["kkn"], t["aj"], ALU.mult, ["kkn", "aj"], ["beta"])
            STT(P, t["t2"], t["rj"], V(7, j), t["k2"], ALU.mult, ALU.mult, ["rj", "vecs", "k2"], ["t2"])
            P.op("pe", lambda e, t=t, p=p: e.matmul(p["pB"], bo[:], t["t2"], start=True, stop=True), reads=["bo", "t2"], writes=["pB"])
            TT(P, "dve", t["bon"], p["pB"], t["vj"], ALU.mult, ["pB", "vj"], ["bon"])
            P.dma("sp", lambda e, t=t, j=j, t0=t0, n=n: e.dma_start(out=bon_o[j * 128:(j + 1) * 128, t0:t0 + n], in_=t["bon"]), "stbon",
                  reads=["bon"], writes=[("bon_o", j, g)])
            outk.append(("bon_o", j, g))
            chs = chunks_in(t0, n)
            for (c0, cn) in chs:
                l0 = c0 - t0
                P.op("dve", lambda e, l0=l0, cn=cn: e.tensor_tensor_scan(out=tm["cs"][:, l0:l0 + cn], data0=onesb[:, :cn], data1=tm["sw"][:, l0:l0 + cn],
                                                                         initial=0.0, op0=ALU.mult, op1=ALU.add),
                     reads=["onesb", "sw"], writes=["cs"])
            TT(P, "pool", t["ex"], t["cs"], t["sw"], ALU.subtract, ["cs", "sw"], ["ex"])
            ACT(P, t["E1"], t["cs"], AF.Exp, ["cs"], ["E1"], scale=-DEC)
            ACT(P, t["E2"], t["ex"], AF.Exp, ["ex"], ["E2"], scale=-DEC)
            ACT(P, t["E3"], t["cs"], AF.Exp, ["cs"], ["E3"], scale=DEC)
            for (c0, cn) in chs:
                l0 = c0 - t0
                ch = c0 // 128
                TS(P, "dve", nb[:, ch:ch + 1], tm["cs"][:, l0 + cn - 1:l0 + cn], -DEC, None, ALU.mult, None, ["cs"], [("nb", ch)])
                ACT(P, tm["E4"][:, l0:l0 + cn], tm["cs"][:, l0:l0 + cn], AF.Exp, ["cs", ("nb", ch)], ["E4"], bias=nb[:, ch:ch + 1], scale=DEC)
                P.op("pool", lambda e, j=j, ch=ch, l0=l0, cn=cn: e.tensor_copy(out=wcb[:, j * NCH + ch:j * NCH + ch + 1], in_=tm["E1"][:, l0 + cn - 1:l0 + cn]),
                     reads=["E1"], writes=[("wcb", j, ch)])
            TT(P, "pool", o["ra"], t["rj"], t["E1"], ALU.mult, ["rj", "E1"], ["o_ra"])
            STT(P, o["aa"], t["kkn"], -1.0, t["E2"], ALU.mult, ALU.mult, ["kkn", "E2"], ["o_aa"])
            TT(P, "pool", o["bb"], t["beta"], t["E3"], ALU.mult, ["beta", "E3"], ["o_bb"])
            TT(P, "pool", o["kt"], t["k2"], t["E3"], ALU.mult, ["k2", "E3"], ["o_kt"])
            TT(P, "pool", o["bh"], t["beta"], t["E4"], ALU.mult, ["beta", "E4"], ["o_bh"])
            TT(P, "dve", o["kh"], t["k2"], t["E4"], ALU.mult, ["k2", "E4"], ["o_kh"])
            ACT(P, o["vv"], t["vj"], AF.Copy, ["vj"], ["o_vv"])
            for nm in names:
                P.dma("sp", lambda e, nm=nm, o=o, j=j, t0=t0, n=n: e.dma_start(out=outs[nm][j * 128:(j + 1) * 128, t0:t0 + n], in_=o[nm]), "st_" + nm,
                      reads=["o_" + nm], writes=[("out_" + nm, j, g)])
                outk.append(("out_" + nm, j, g))
    P.dma("sp", lambda e: e.dma_start(out=wc_o, in_=wcb[:]), "stwc", reads=[("wcb", j, ch) for j in range(DT) for ch in range(NCH)], writes=["wc_o"])
    outk.append("wc_o")
    P.finish(outk)
    P.emit()
    return nc


def blockones():
    c = np.zeros((128, 128), np.float32)
    c[:64, :64] = 1.0
    c[64:, 64:] = 1.0
    return c


def rwkv_main_inputs(I, hsh, halo, tw, ta):
    mu = I['rw_mu'][0]
    vecs = np.ascontiguousarray(np.concatenate(
        [pv(mu[0]), pv(mu[2]), pv(mu[3]), pv(I['rw_w0'][0]), pv(I['rw_a0'][0]), pv(I['rw_k_k'][0]), pv(I['rw_k_a'][0]),
         pv(I['rw_r_k'][0].reshape(-1))], axis=1))
    return {"h": hsh, "halo": halo, "tw": tw, "ta": ta, "wrkv": I['rw_w_rkv'][0], "w2": I['rw_w2'][0], "a2": I['rw_a2'][0],
            "vecs": vecs, "cst": blockones()}


NHS = 16
SCH = [(half * T + c0, cn) for half in range(2) for (c0, cn) in chunks_in(0, T)]
NSC = len(SCH)


def build_scan():
    nc = new_nc()
    aaF = din(nc, "aaF", [64, NHS, L], BF16)
    raF = din(nc, "raF", [64, NHS, L], BF16)
    bbF = din(nc, "bbF", [64, NHS, L], BF16)
    ktF = din(nc, "ktF", [64, NHS, L], BF16)
    bhT = din(nc, "bhT", [L, NHS * 64], BF16)
    khT = din(nc, "khT", [L, NHS * 64], BF16)
    vvT = din(nc, "vvT", [L, NHS * 64], BF16)
    wc = din(nc, "wc", [64, NHS * NSC])
    cst = din(nc, "cst", [128, 3 * 512])
    yo = dout(nc, "yo", [NHS * 64, L])
    P = Prog(nc)
    fa = [P.sbuf(f"fa{i}", [64, NHS, 2, 128], BF16) for i in range(2)]
    fb = [P.sbuf(f"fb{i}", [64, NHS, 128], BF16) for i in range(2)]
    fk = [P.sbuf(f"fk{i}", [64, NHS, 128], BF16) for i in range(2)]
    tb = [P.sbuf(f"tb{i}", [128, NHS * 64], BF16) for i in range(2)]
    tk = [P.sbuf(f"tk{i}", [128, NHS * 64], BF16) for i in range(2)]
    tv = [P.sbuf(f"tv{i}", [128, NHS * 64], BF16) for i in range(2)]
    MB = [P.sbuf(f"MB{i}", [128, NHS, 2, 128], BF16) for i in range(2)]
    MK = [P.sbuf(f"MK{i}", [128, NHS, 2, 128], BF16) for i in range(2)]
    NT = P.sbuf("NT", [128, NHS, 128], BF16)
    Rf = [P.sbuf(f"Rf{i}", [128, NHS, 128], BF16) for i in range(2)]
    Ab = [P.sbuf(f"Ab{i}", [128, 4, 128], BF16) for i in range(2)]
    ATb = [P.sbuf(f"ATb{i}", [128, 4, 128], BF16) for i in range(2)]
    Rb = [P.sbuf(f"Rb{i}", [128, 4, 128], BF16) for i in range(2)]
    Xb = P.sbuf("Xb", [128, 4, 64], BF16)
    Ub = P.sbuf("Ub", [128, NHS, 64], BF16)
    S0 = P.sbuf("S0", [64, NHS, 64], F32)
    S0b = P.sbuf("S0b", [64, NHS, 64], BF16)
    Yo = [P.sbuf(f"Yo{i}", [64, 4, 128], F32) for i in range(2)]
    ct = P.sbuf("ct", [128, 3 * 512], F32)
    identb = P.sbuf("identb", [128, 128], BF16)
    wcs = P.sbuf("wcs", [64, NHS * NSC], F32)
    bank = [P.psum(f"bank{i}", [128, 512]) for i in range(8)]
    BK = lambda i: ("bank", i)
    mUU = ct[:, 0:512].rearrange("p (a b c) -> p a b c", a=2, b=2)
    mL = ct[:, 512:1024].rearrange("p (a c) -> p a c", a=4)
    id4 = ct[:, 1024:1536].rearrange("p (a c) -> p a c", a=4)
    psB = bank[0][:].rearrange("p (a b c) -> p a b c", a=2, b=2)
    psK = bank[1][:].rearrange("p (a b c) -> p a b c", a=2, b=2)
    psN = bank[2][:].rearrange("p (a c) -> p a c", a=4)
    psA2 = bank[3][:].rearrange("p (a c) -> p a c", a=4)
    psAT2 = bank[4][:].rearrange("p (a c) -> p a c", a=4)
    psR = bank[5][:].rearrange("p (a c) -> p a c", a=4)
    psXU = bank[6]
    psY = bank[7][:].rearrange("p (a c) -> p a c", a=4)
    P.dma("sp", lambda e: e.dma_start(out=ct[:], in_=cst), "ldc", writes=["ct"])
    P.dma("sp", lambda e: e.dma_start(out=wcs[:], in_=wc), "ldwc", writes=["wcs"])
    ACT(P, identb[:], ct[:, 1024:1152], AF.Copy, ["ct"], ["identb"])
    P.op("pool", lambda e: e.memset(S0[:], 0.0), writes=[("S0", g4) for g4 in range(4)])
    P.op("pool", lambda e: e.memset(S0b[:], 0.0), writes=[("S0b", g4) for g4 in range(4)])
    outk = []
    for c, (t0, C) in enumerate(SCH):
        s = c % 2
        fa_t, fb_t, fk_t, tb_t, tk_t, tv_t = fa[s], fb[s], fk[s], tb[s], tk[s], tv[s]
        MBc, MKc, Rfc = MB[s], MK[s], Rf[s]
        P.dma("sp", lambda e, fa_t=fa_t, t0=t0, C=C: e.dma_start(out=fa_t[:, :, 0, :C], in_=aaF[:, :, t0:t0 + C]), f"lfa{s}", writes=[("fa0", s)])
        P.dma("sp", lambda e, fa_t=fa_t, t0=t0, C=C: e.dma_start(out=fa_t[:, :, 1, :C], in_=raF[:, :, t0:t0 + C]), f"lfr{s}", writes=[("fa1", s)])
        P.dma("sp", lambda e, fb_t=fb_t, t0=t0, C=C: e.dma_start(out=fb_t[:, :, :C], in_=bbF[:, :, t0:t0 + C]), f"lfb{s}", writes=[("fb", s)])
        P.dma("sp", lambda e, fk_t=fk_t, t0=t0, C=C: e.dma_start(out=fk_t[:, :, :C], in_=ktF[:, :, t0:t0 + C]), f"lfk{s}", writes=[("fk", s)])
        P.dma("sp", lambda e, tb_t=tb_t, t0=t0, C=C: e.dma_start(out=tb_t[:C, :], in_=bhT[t0:t0 + C, :]), f"ltb{s}", writes=[("tb", s)])
        P.dma("sp", lambda e, tk_t=tk_t, t0=t0, C=C: e.dma_start(out=tk_t[:C, :], in_=khT[t0:t0 + C, :]), f"ltk{s}", writes=[("tk", s)])
        P.dma("sp", lambda e, tv_t=tv_t, t0=t0, C=C: e.dma_start(out=tv_t[:C, :], in_=vvT[t0:t0 + C, :]), f"ltv{s}", writes=[("tv", s)])
        FA = [("fa0", s), ("fa1", s)]
        for hp in range(NHS // 2):
            def mm1(e, hp=hp, C=C, fa_t=fa_t, fb_t=fb_t):
                for a in range(2):
                    h = 2 * hp + a
                    if C == 128:
                        ins = e.matmul(psB[:C, a, :, :C], fb_t[:, h, :C], fa_t[:, h, :, :C], start=True, stop=True)
                    else:
                        for b2 in range(2):
                            ins = e.matmul(psB[:C, a, b2, :C], fb_t[:, h, :C], fa_t[:, h, b2, :C], start=True, stop=True)
                return ins

            def mm2(e, hp=hp, C=C, fa_t=fa_t, fk_t=fk_t):
                for a in range(2):
                    h = 2 * hp + a
                    if C == 128:
                        ins = e.matmul(psK[:C, a, :, :C], fk_t[:, h, :C], fa_t[:, h, :, :C], start=True, stop=True)
                    else:
                        for b2 in range(2):
                            ins = e.matmul(psK[:C, a, b2, :C], fk_t[:, h, :C], fa_t[:, h, b2, :C], start=True, stop=True)
                return ins
            P.op("pe", mm1, reads=FA + [("fb", s)], writes=[BK(0)])
            P.op("pe", mm2, reads=FA + [("fk", s)], writes=[BK(1)])
            TT(P, "dve", MBc[:C, 2 * hp:2 * hp + 2, :, :C], psB[:C, :, :, :C], mUU[:C, :, :, :C], ALU.mult, [BK(0), "ct"], [("MB", s, hp // 2)])
            TT(P, "dve", MKc[:C, 2 * hp:2 * hp + 2, :, :C], psK[:C, :, :, :C], mUU[:C, :, :, :C], ALU.mult, [BK(1), "ct"], [("MK", s, hp // 2)])
        nsq = max(1, int(np.ceil(np.log2(C))) - 1)
        for g4 in range(NHS // 4):
            h0 = 4 * g4

            def mmn(e, h0=h0, C=C, fa_t=fa_t, fb_t=fb_t):
                for a in range(4):
                    ins = e.matmul(psN[:C, a, :C], fa_t[:, h0 + a, 0, :C], fb_t[:, h0 + a, :C], start=True, stop=True)
                return ins
            P.op("pe", mmn, reads=FA + [("fb", s)], writes=[BK(2)])
            TT(P, "dve", NT[:C, h0:h0 + 4, :C], psN[:C, :, :C], mL[:C, :, :C], ALU.mult, [BK(2), "ct"], [("NT", g4)])
            ri = P.rot("Rb")
            TT(P, "pool", Rb[ri][:C, :, :C], MBc[:C, h0:h0 + 4, 0, :C], id4[:C, :, :C], ALU.add, [("MB", s, g4), "ct"], [("Rb", ri)])
            A_cur, AT_cur = None, None
            A_key, AT_key = ("MB", s, g4), ("NT", g4)
            for i in range(nsq):
                last = (i == nsq - 1)
                if A_cur is None:
                    Aap = lambda a, MBc=MBc, h0=h0, C=C: MBc[:C, h0 + a, 0, :C]
                    ATap = lambda a, h0=h0, C=C: NT[:C, h0 + a, :C]
                else:
                    Aap = lambda a, A_cur=A_cur, C=C: A_cur[:C, a, :C]
                    ATap = lambda a, AT_cur=AT_cur, C=C: AT_cur[:C, a, :C]
                ai = P.rot("Ab")
                if not last:
                    def mma(e, Aap=Aap, ATap=ATap, C=C):
                        for a in range(4):
                            ins = e.matmul(psA2[:C, a, :C], ATap(a), Aap(a), start=True, stop=True)
                        return ins
                    P.op("pe", mma, reads=[A_key, AT_key], writes=[BK(3)])
                    ACT(P, Ab[ai][:C, :, :C], psA2[:C, :, :C], AF.Copy, [BK(3)], [("Ab", ai)])

                def mmat(e, Aap=Aap, ATap=ATap, C=C):
                    for a in range(4):
                        ins = e.matmul(psAT2[:C, a, :C], Aap(a), ATap(a), start=True, stop=True)
                    return ins
                P.op("pe", mmat, reads=[A_key, AT_key], writes=[BK(4)])
                P.op("dve", lambda e, ai=ai, C=C: e.tensor_copy(out=ATb[ai][:C, :, :C], in_=psAT2[:C, :, :C]), reads=[BK(4)], writes=[("ATb", ai)])
                Rcur = Rb[ri]

                def mmr(e, Rcur=Rcur, ATn=ATb[ai], C=C):
                    for a in range(4):
                        e.matmul(psR[:C, a, :C], identb[:C, :C], Rcur[:C, a, :C], start=True, stop=False)
                        ins = e.matmul(psR[:C, a, :C], ATn[:C, a, :C], Rcur[:C, a, :C], start=False, stop=True)
                    return ins
                P.op("pe", mmr, reads=[("Rb", ri), ("ATb", ai), "identb"], writes=[BK(5)])
                if last:
                    ACT(P, Rfc[:C, h0:h0 + 4, :C], psR[:C, :, :C], AF.Copy, [BK(5)], [("Rf", s, g4)])
                else:
                    ri = P.rot("Rb")
                    ACT(P, Rb[ri][:C, :, :C], psR[:C, :, :C], AF.Copy, [BK(5)], [("Rb", ri)])
                A_cur, AT_cur = Ab[ai], ATb[ai]
                A_key, AT_key = ("Ab", ai), ("ATb", ai)
        for g4 in range(NHS // 4):
            h0 = 4 * g4

            def mmx(e, h0=h0, C=C, fa_t=fa_t, MKc=MKc, tv_t=tv_t):
                for a in range(4):
                    h = h0 + a
                    e.matmul(psXU[:C, a * 64:(a + 1) * 64], fa_t[:, h, 0, :C], S0b[:, h, :], start=True, stop=False)
                    ins = e.matmul(psXU[:C, a * 64:(a + 1) * 64], MKc[:C, h, 0, :C], tv_t[:C, h * 64:(h + 1) * 64], start=False, stop=True)
                return ins
            P.op("pe", mmx, reads=FA + [("S0b", g4), ("MK", s, g4), ("tv", s)], writes=[BK(6)])
            ACT(P, Xb[:C, :, :], psXU[:C, 0:256].rearrange("p (a c) -> p a c", a=4), AF.Copy, [BK(6)], ["Xb"])

            def mmu(e, h0=h0, C=C, Rfc=Rfc):
                for a in range(4):
                    ins = e.matmul(psXU[:C, a * 64:(a + 1) * 64], Rfc[:C, h0 + a, :C], Xb[:C, a, :], start=True, stop=True)
                return ins
            P.op("pe", mmu, reads=[("Rf", s, g4), "Xb"], writes=[BK(6)])
            P.op("dve", lambda e, h0=h0, C=C: e.tensor_copy(out=Ub[:C, h0:h0 + 4, :], in_=psXU[:C, 0:256].rearrange("p (a c) -> p a c", a=4)),
                 reads=[BK(6)], writes=[("Ub", g4)])

            def mmy(e, h0=h0, C=C, fa_t=fa_t, MBc=MBc, MKc=MKc, tv_t=tv_t):
                for a in range(4):
                    h = h0 + a
                    e.matmul(psY[:64, a, :C], S0b[:, h, :], fa_t[:, h, 1, :C], start=True, stop=False)
                    e.matmul(psY[:64, a, :C], Ub[:C, h, :], MBc[:C, h, 1, :C], start=False, stop=False)
                    ins = e.matmul(psY[:64, a, :C], tv_t[:C, h * 64:(h + 1) * 64], MKc[:C, h, 1, :C], start=False, stop=True)
                return ins
            P.op("pe", mmy, reads=FA + [("S0b", g4), ("Ub", g4), ("MB", s, g4), ("MK", s, g4), ("tv", s)], writes=[BK(7)])
            yi = P.rot("Yo")
            ACT(P, Yo[yi][:, :, :C], psY[:64, :, :C], AF.Copy, [BK(7)], [("Yo", yi)])
            P.dma("sp", lambda e, yi=yi, h0=h0, t0=t0, C=C: e.dma_start(
                out=yo[h0 * 64:(h0 + 4) * 64, t0:t0 + C].rearrange("(a v) t -> v a t", v=64), in_=Yo[yi][:, :, :C]), f"sty{yi}",
                reads=[("Yo", yi)], writes=[("yo", c, g4)])
            outk.append(("yo", c, g4))

            def mmS(e, h0=h0, C=C, tb_t=tb_t, tk_t=tk_t, tv_t=tv_t):
                for a in range(4):
                    h = h0 + a
                    e.matmul(psXU[:64, 256 + a * 64:256 + (a + 1) * 64], tb_t[:C, h * 64:(h + 1) * 64], Ub[:C, h, :], start=True, stop=False)
                    ins = e.matmul(psXU[:64, 256 + a * 64:256 + (a + 1) * 64], tk_t[:C, h * 64:(h + 1) * 64], tv_t[:C, h * 64:(h + 1) * 64], start=False, stop=True)
                return ins
            P.op("pe", mmS, reads=[("tb", s), ("tk", s), ("tv", s), ("Ub", g4)], writes=[BK(6)])
            for a in range(4):
                h = h0 + a
                STT(P, S0[:, h, :], S0[:, h, :], wcs[:, h * NSC + c:h * NSC + c + 1], psXU[:64, 256 + a * 64:256 + (a + 1) * 64], ALU.mult, ALU.add,
                    [("S0", g4), "wcs", BK(6)], [("S0", g4)])
            P.op("pool", lambda e, h0=h0: e.tensor_copy(out=S0b[:, h0:h0 + 4, :], in_=S0[:, h0:h0 + 4, :]), reads=[("S0", g4)], writes=[("S0b", g4)])
    P.finish(outk)
    P.emit()
    return nc


def scan_consts():
    c = np.zeros((128, 3 * 512), np.float32)
    su = np.triu(np.ones((128, 128), np.float32), 1)
    iu = np.triu(np.ones((128, 128), np.float32), 0)
    c[:, 0:512] = np.concatenate([su, iu, su, iu], axis=1)
    sl = np.tril(np.ones((128, 128), np.float32), -1)
    c[:, 512:1024] = np.concatenate([sl] * 4, axis=1)
    c[:, 1024:1536] = np.concatenate([np.eye(128, dtype=np.float32)] * 4, axis=1)
    return c


def stage_rwkv(h_sh, I):
    mu = I['rw_mu'][0]
    halos = []
    for c in range(NC):
        if c % 2 == 0:
            halos.append(np.zeros((D, 1), np.float32))
        else:
            halos.append(np.ascontiguousarray(h_sh[c - 1][:, T - 1:T]))
    nc = get_nc("rw_lora", build_rwkv_lora)
    vecs = np.ascontiguousarray(np.concatenate([pv(mu[1]), pv(mu[4]), pv(mu[5])], axis=1))
    res = run(nc, [{"h": h_sh[c], "halo": halos[c], "vecs": vecs, "w1": I['rw_w1'][0], "a1": I['rw_a1'][0],
                    "g1": I['rw_g1'][0], "g2": I['rw_g2'][0]} for c in range(NC)])
    gg = [r["gg"] for r in res]
    nc = get_nc("rw_main", build_rwkv_main)
    resm = run(nc, [rwkv_main_inputs(I, h_sh[c], halos[c], res[c]["tw"], res[c]["ta"]) for c in range(NC)])
    def full(nm):
        return [np.concatenate([resm[2 * b][nm], resm[2 * b + 1][nm]], axis=1) for b in range(B)]
    F = {nm: full(nm) for nm in ["ra", "aa", "bb", "kt", "bh", "kh", "vv"]}
    wcf = []
    for b in range(B):
        w = np.concatenate([resm[2 * b + hf]["wc"].reshape(128, DT, NCH) for hf in range(2)], axis=2)
        wcf.append(w.transpose(1, 0, 2).reshape(D, NSC))
    cst = scan_consts()
    maps = []
    for c in range(NC):
        b, hh = c // 2, c % 2
        rows = slice(hh * NHS * 64, (hh + 1) * NHS * 64)

        def FM(nm):
            return np.ascontiguousarray(F[nm][b][rows].reshape(NHS, 64, L).transpose(1, 0, 2))

        def TM(nm):
            return np.ascontiguousarray(F[nm][b][rows].T)
        wcc = np.ascontiguousarray(wcf[b][rows].reshape(NHS, 64, NSC).transpose(1, 0, 2).reshape(64, NHS * NSC))
        maps.append({"aaF": FM("aa"), "raF": FM("ra"), "bbF": FM("bb"), "ktF": FM("kt"),
                     "bhT": TM("bh"), "khT": TM("kh"), "vvT": TM("vv"), "wc": wcc, "cst": cst})
    nc = get_nc("scan", build_scan)
    ress = run(nc, maps)
    ys = []
    for c in range(NC):
        b, hf = c // 2, c % 2
        yfull = np.concatenate([ress[2 * b]["yo"], ress[2 * b + 1]["yo"]], axis=0)
        ys.append(np.ascontiguousarray(yfull[:, hf * T:(hf + 1) * T]))
    nc = get_nc("post_rwkv", build_post, True)
    vecs = np.ascontiguousarray(np.concatenate([pv(I['ln_g'][0, 1]), pv(I['ln_b'][0, 1]), pv(I['rw_lnx_g'][0]), pv(I['rw_lnx_b'][0])], axis=1))
    bo = blockones() / 64.0
    resp = run(nc, [{"h": h_sh[c], "wo": I['rw_w_o'][0], "vecs": vecs, "ys": ys[c], "bon": resm[c]["bon"], "gg": gg[c], "cst": bo}
                    for c in range(NC)])
    return [r["y"] for r in resp]


def kernel(**I):
    I = {k: np.asarray(v) for k, v in I.items()}
    x = I['x']
    meta = np.broadcast_to(I['meta_tokens'][None], (B, NMETA, D))
    h = np.concatenate([meta, x], axis=1).astype(np.float32)
    hs = to_fm(h)
    hs = stage_ffn(hs, I['ffn_w1'][0, 0], I['ffn_w3'][0, 0], I['ffn_w2'][0, 0], I['ln_g'][0, 0], I['ln_b'][0, 0])
    hs = stage_rwkv(hs, I)
    hs = stage_ffn(hs, I['ffn_w1'][0, 1], I['ffn_w3'][0, 1], I['ffn_w2'][0, 1], I['ln_g'][0, 2], I['ln_b'][0, 2])
    kv_sh, sp_sh = stage_linear(hs, I['fx_w_kvf'], 2 * D, bf=I['fx_b_f'])
    hs = stage_ffn(hs, I['ffn_w1'][1, 0], I['ffn_w3'][1, 0], I['ffn_w2'][1, 0], I['ln_g'][1, 0], I['ln_b'][1, 0])
    q_sh = stage_linear(hs, I['fx_w_q'][0], D)
    u_sh = stage_attn(q_sh, kv_sh, sp_sh)
    hs = stage_post(hs, u_sh, I['fx_w_o'][0], I['ln_g'][1, 1], I['ln_b'][1, 1])
    hs = stage_ffn(hs, I['ffn_w1'][1, 1], I['ffn_w3'][1, 1], I['ffn_w2'][1, 1], I['ln_g'][1, 2], I['ln_b'][1, 2])
    out = from_fm(hs)
    return np.ascontiguousarray(out[:, NMETA:, :]).astype(np.float32)
```
